# Optimizing a Trainium2 kernel written in Bass

```python
import jax, jax.numpy as jnp
from jax import lax
import numpy as np

D_MODEL = 1024
BATCH = 16
SEQ = 2048
DEPTH = 4

POOL_GROUPS = 4
POOL_WINDOWS = (2, 4, 8, 16)
POOL_GROUP_DIM = D_MODEL // 8
POOL_WIDTH = POOL_GROUPS * POOL_GROUP_DIM
POOL_OUT_GROUP = D_MODEL // POOL_GROUPS
CONV_WIDTH = D_MODEL // 2
CONV_K = 31
SGU_HEADS = 4
SGU_WIDTH = D_MODEL // 2
SGU_HEAD_DIM = SGU_WIDTH // SGU_HEADS
SGU_CHUNK = 128
N_BRANCH = 3
D_FF = 2816
FFN_CONV_K = 3
RMS_EPS = 1e-6
LN_EPS = 1e-5

IN_SPLITS = (
    POOL_WIDTH,
    POOL_WIDTH + CONV_WIDTH,
    POOL_WIDTH + 2 * CONV_WIDTH,
    POOL_WIDTH + 2 * CONV_WIDTH + SGU_WIDTH,
    POOL_WIDTH + 2 * CONV_WIDTH + 2 * SGU_WIDTH,
)
IN_COLS = POOL_WIDTH + 2 * CONV_WIDTH + 2 * SGU_WIDTH + N_BRANCH * D_MODEL

kernel_name = 'hybrid_pool_conv_sgu_gated_trunk'


def rmsnorm(x, g):
    xf = x.astype(jnp.float32)
    y = xf * lax.rsqrt(jnp.mean(xf * xf, axis=-1, keepdims=True) + RMS_EPS)
    return (y * g.astype(jnp.float32)).astype(x.dtype)


def layernorm(x, g, b):
    xf = x.astype(jnp.float32)
    mu = jnp.mean(xf, axis=-1, keepdims=True)
    var = jnp.mean(jnp.square(xf - mu), axis=-1, keepdims=True)
    y = (xf - mu) * lax.rsqrt(var + LN_EPS)
    return (y * g.astype(jnp.float32) + b.astype(jnp.float32)).astype(x.dtype)


def causal_dwconv(x, w, b):
    k, c = w.shape
    y = lax.conv_general_dilated(
        x, w[:, None, :].astype(x.dtype), window_strides=(1,), padding=[(k - 1, 0)],
        dimension_numbers=('NWC', 'WIO', 'NWC'), feature_group_count=c)
    return y + b.astype(x.dtype)


def pool_mixer(xa, w_grp, scale):
    b, s, _ = xa.shape
    xg = xa.reshape(b, s, POOL_GROUPS, POOL_GROUP_DIM).astype(jnp.float32)
    cs = jnp.cumsum(xg, axis=1)
    wmax = max(POOL_WINDOWS)
    cs_pad = jnp.pad(cs, ((0, 0), (wmax, 0), (0, 0), (0, 0)))
    pos = jnp.arange(s)
    win = jnp.array(POOL_WINDOWS, dtype=jnp.int32)
    idx = pos[:, None] - win[None, :] + wmax
    lower = cs_pad[:, idx, jnp.arange(POOL_GROUPS)[None, :], :]
    cnt = jnp.minimum(pos[:, None] + 1, win[None, :]).astype(jnp.float32)
    pooled = (cs - lower) / cnt[None, :, :, None] - xg
    pooled = pooled.astype(xa.dtype)
    y = jnp.einsum('bsgc,gcd->bsgd', pooled, w_grp)
    return y.reshape(b, s, D_MODEL) * scale


def conv_module(val, gate, w_dw, b_dw, ln_g, ln_b, w_pw):
    h = val * jax.nn.sigmoid(gate)
    h = causal_dwconv(h, w_dw, b_dw)
    h = jax.nn.silu(layernorm(h, ln_g, ln_b))
    return h @ w_pw


def sgu_mixer(u, v, w_s, b_s, ln_g, ln_b, w_o):
    b, s, _ = u.shape
    n_chunk = s // SGU_CHUNK
    u = jax.nn.gelu(u)
    v = layernorm(jax.nn.gelu(v), ln_g, ln_b)
    vc = v.reshape(b, n_chunk, SGU_CHUNK, SGU_HEADS, SGU_HEAD_DIM)
    mask = jnp.tril(jnp.ones((SGU_CHUNK, SGU_CHUNK), dtype=bool))
    ws = jnp.where(mask[None], w_s, 0.0).astype(v.dtype)
    mixed = jnp.einsum('hts,bnshe->bnthe', ws, vc) + b_s.T.astype(v.dtype)[None, None, :, :, None]
    gated = u * mixed.reshape(b, s, SGU_WIDTH)
    return gated @ w_o


def conv_ffn(h, w_up, w_dw, b_dw, w_down):
    z = h @ w_up
    g, val = jnp.split(z, 2, axis=-1)
    g = causal_dwconv(g, w_dw, b_dw)
    return (jax.nn.silu(g) * val) @ w_down


def setup_inputs(seed: int = 0) -> dict:
    key = jax.random.key(seed)
    ks = jax.random.split(key, 24)
    L = DEPTH
    f32 = jnp.float32

    def nrm(k, shape, scale):
        return jax.random.normal(k, shape, dtype=f32) * scale

    return {
        'x': nrm(ks[0], (BATCH, SEQ, D_MODEL), 1.0),
        'norm_mix': 1.0 + nrm(ks[1], (L, D_MODEL), 0.02),
        'w_in': nrm(ks[2], (L, D_MODEL, IN_COLS), D_MODEL ** -0.5),
        'b_gate': nrm(ks[3], (L, N_BRANCH * D_MODEL), 0.02),
        'pool_w': nrm(ks[4], (L, POOL_GROUPS, POOL_GROUP_DIM, POOL_OUT_GROUP), POOL_GROUP_DIM ** -0.5),
        'pool_scale': 1.0 + nrm(ks[5], (L, D_MODEL), 0.02),
        'conv_dw_w': nrm(ks[6], (L, CONV_K, CONV_WIDTH), CONV_K ** -0.5),
        'conv_dw_b': nrm(ks[7], (L, CONV_WIDTH), 0.02),
        'conv_ln_g': 1.0 + nrm(ks[8], (L, CONV_WIDTH), 0.02),
        'conv_ln_b': nrm(ks[9], (L, CONV_WIDTH), 0.02),
        'conv_pw': nrm(ks[10], (L, CONV_WIDTH, D_MODEL), CONV_WIDTH ** -0.5),
        'sgu_w': nrm(ks[11], (L, SGU_HEADS, SGU_CHUNK, SGU_CHUNK), SGU_CHUNK ** -0.5),
        'sgu_b': 1.0 + nrm(ks[12], (L, SGU_HEADS, SGU_CHUNK), 0.02),
        'sgu_ln_g': 1.0 + nrm(ks[13], (L, SGU_WIDTH), 0.02),
        'sgu_ln_b': nrm(ks[14], (L, SGU_WIDTH), 0.02),
        'sgu_out': nrm(ks[15], (L, SGU_WIDTH, D_MODEL), SGU_WIDTH ** -0.5),
        'w_out': nrm(ks[16], (L, D_MODEL, D_MODEL), D_MODEL ** -0.5),
        'norm_ffn': 1.0 + nrm(ks[17], (L, D_MODEL), 0.02),
        'ffn_up': nrm(ks[18], (L, D_MODEL, 2 * D_FF), D_MODEL ** -0.5),
        'ffn_dw_w': nrm(ks[19], (L, FFN_CONV_K, D_FF), FFN_CONV_K ** -0.5),
        'ffn_dw_b': nrm(ks[20], (L, D_FF), 0.02),
        'ffn_down': nrm(ks[21], (L, D_FF, D_MODEL), D_FF ** -0.5),
        'norm_final': 1.0 + nrm(ks[22], (D_MODEL,), 0.02),
    }


def reference(x, norm_mix, w_in, b_gate, pool_w, pool_scale, conv_dw_w, conv_dw_b,
              conv_ln_g, conv_ln_b, conv_pw, sgu_w, sgu_b, sgu_ln_g, sgu_ln_b, sgu_out,
              w_out, norm_ffn, ffn_up, ffn_dw_w, ffn_dw_b, ffn_down, norm_final):
    b, s, _ = x.shape
    for l in range(DEPTH):
        h = rmsnorm(x, norm_mix[l])
        z = h @ w_in[l]
        z_pool, z_cv, z_cg, z_u, z_v, z_gate = jnp.split(z, IN_SPLITS, axis=-1)
        y_a = pool_mixer(z_pool, pool_w[l], pool_scale[l])
        y_b = conv_module(z_cv, z_cg, conv_dw_w[l], conv_dw_b[l], conv_ln_g[l], conv_ln_b[l], conv_pw[l])
        y_c = sgu_mixer(z_u, z_v, sgu_w[l], sgu_b[l], sgu_ln_g[l], sgu_ln_b[l], sgu_out[l])
        gates = jax.nn.sigmoid(z_gate + b_gate[l]).reshape(b, s, N_BRANCH, D_MODEL)
        merged = gates[:, :, 0] * y_a + gates[:, :, 1] * y_b + gates[:, :, 2] * y_c
        x = x + merged @ w_out[l]
        h = rmsnorm(x, norm_ffn[l])
        x = x + conv_ffn(h, ffn_up[l], ffn_dw_w[l], ffn_dw_b[l], ffn_down[l])
    return rmsnorm(x, norm_final)
```

```python
import numpy as np
from contextlib import ExitStack
import concourse.bass as bass
import concourse.mybir as mybir
from concourse.bass_utils import run_bass_kernel_spmd

F32 = mybir.dt.float32
BF16 = mybir.dt.bfloat16
AF = mybir.ActivationFunctionType
ALU = mybir.AluOpType

D = 1024
CB = 512
INC = 5632
DFF = 2816
NFT = 22
ENGS = ("pe", "act", "dve", "pool", "sp")

PV_NMIX, PV_BGATE, PV_PSCALE, PV_CDW, PV_CDB, PV_CLG, PV_CLB, PV_NFFN, PV_FDW, PV_FDB = (
    0, 8, 32, 40, 164, 168, 172, 176, 184, 250)
PV_L = 272


class Prog:
    def __init__(self):
        self.ops = {e: [] for e in ENGS}
        self.last_write = {}
        self.readers = {}
        self.known = {e: {} for e in ENGS}
        self.dma_count = {}
        self.snap = {}
        self.signal = set()
        self.label = ""
        self.labels = {e: [] for e in ENGS}

    def op(self, eng, fn, reads=(), writes=(), dma_slot=None):
        idx = len(self.ops[eng])
        self.labels[eng].append(self.label)
        deps = {}
        raw = {}
        for r in reads:
            lw = self.last_write.get(r)
            if lw is not None:
                if deps.get(lw[0], -1) < lw[1]:
                    deps[lw[0]] = lw[1]
                if raw.get(lw[0], -1) < lw[1]:
                    raw[lw[0]] = lw[1]
        for r in writes:
            lw = self.last_write.get(r)
            if lw is not None and deps.get(lw[0], -1) < lw[1]:
                deps[lw[0]] = lw[1]
            for a, i in self.readers.get(r, {}).items():
                if deps.get(a, -1) < i:
                    deps[a] = i
        if dma_slot is not None:
            n_prev = self.dma_count.get(dma_slot, 0)
            if n_prev > 0:
                a = "dma:%s" % dma_slot
                if deps.get(a, -1) < n_prev - 1:
                    deps[a] = n_prev - 1
        waits = []
        for a, i in deps.items():
            if a == eng:
                if eng in ("act", "dve", "pool") and dma_slot is None and raw.get(a, -1) >= idx - 1:
                    waits.append((a, raw[a]))
                continue
            if self.known[eng].get(a, -1) >= i:
                continue
            waits.append((a, i))
        k = self.known[eng]
        for a, i in waits:
            if not a.startswith("dma:"):
                self.signal.add((a, i))
            if a != eng:
                if k.get(a, -1) < i:
                    k[a] = i
                for a2, i2 in self.snap.get((a, i), {}).items():
                    if a2 != eng and k.get(a2, -1) < i2:
                        k[a2] = i2
        self.ops[eng].append((fn, waits, dma_slot))
        if dma_slot is not None:
            n = self.dma_count.get(dma_slot, 0)
            self.dma_count[dma_slot] = n + 1
            actor, aidx = "dma:%s" % dma_slot, n
        else:
            actor, aidx = eng, idx
        snap = dict(k)
        if dma_slot is None:
            snap[eng] = idx
        self.snap[(actor, aidx)] = snap
        for r in writes:
            self.last_write[r] = (actor, aidx)
            self.readers[r] = {}
        for r in reads:
            self.readers.setdefault(r, {})[actor] = aidx
        return (actor, aidx)

    def wait_for(self, eng, handles):
        waits = []
        for a, i in handles:
            if a == eng or self.known[eng].get(a, -1) >= i:
                continue
            waits.append((a, i))
            if not a.startswith("dma:"):
                self.signal.add((a, i))
            self.known[eng][a] = i
        self.ops[eng].append((None, waits, None))
        self.labels[eng].append(self.label)

    def emit(self, block, sems, dma_sems):
        cnt = {}
        for e in ENGS:
            c = 0
            arr = []
            for i in range(len(self.ops[e])):
                if (e, i) in self.signal:
                    c += 1
                arr.append(c)
            cnt[e] = arr
        self.sig_counts = {e: (cnt[e][-1] if cnt[e] else 0) for e in ENGS}
        self.names = {e: [] for e in ENGS}

        def run(e, engine):
            for i, (fn, waits, dslot) in enumerate(self.ops[e]):
                for a, ai in waits:
                    if a.startswith("dma:"):
                        engine.wait_ge(dma_sems[a[4:]], 16 * (ai + 1))
                    else:
                        engine.wait_ge(sems[a], cnt[a][ai])
                if fn is None:
                    continue
                inst = fn(engine)
                try:
                    self.names[e].append((i, inst.ins.name))
                except Exception:
                    pass
                if dslot is not None:
                    inst.then_inc(dma_sems[dslot], 16)
                elif (e, i) in self.signal:
                    inst.then_inc(sems[e], 1)

        @block.tensor
        def _(eng):
            run("pe", eng)

        @block.scalar
        def _(eng):
            run("act", eng)

        @block.vector
        def _(eng):
            run("dve", eng)

        @block.gpsimd
        def _(eng):
            run("pool", eng)

        @block.sync
        def _(eng):
            run("sp", eng)


NWS = 3


def build_nc(n_seq, n_blk, depth):
    NB = n_seq
    ntok = n_seq * n_blk * CB
    nc = bass.Bass("TRN2", target_bir_lowering=False)

    def din(name, shape):
        return nc.dram_tensor(name, list(shape), F32, kind="ExternalInput").ap()

    xT = din("xT", [D, ntok])
    outT = nc.dram_tensor("outT", [D, ntok], F32, kind="ExternalOutput").ap()
    w_in = din("w_in", [depth, D, INC])
    pool_w = din("pool_w", [depth, 4, 128, 256])
    conv_pw = din("conv_pw", [depth, 512, D])
    sgu_w = din("sgu_w", [depth, 4, 128, 128])
    sgu_out = din("sgu_out", [depth, 512, D])
    w_out = din("w_out", [depth, D, D])
    ffn_up = din("ffn_up", [depth, D, 2 * DFF])
    ffn_down = din("ffn_down", [depth, DFF, D])
    pvec_d = din("pvec", [128, depth * PV_L + 8])
    sgb_d = din("sgb", [depth, 2, 128, 512])
    sbrow_d = din("sbrow", [depth, 1, 512])
    cmat_d = din("cmat", [128, 14, 128])
    c32_d = din("c32", [128, 2, 128])

    with ExitStack() as es:
        def sb(name, shape, dt):
            return es.enter_context(nc.sbuf_tensor(name, list(shape), dt))

        RW = 12416
        xb = [sb("xb%d" % j, [128, 8, CB], F32) for j in range(NB)]
        hT = [sb("hT%d" % j, [128, 8, CB], BF16) for j in range(NB)]
        R = [sb("R%d" % j, [128, RW], BF16) for j in range(NB)]

        def rv(j, off, k, c):
            return R[j][:, off:off + k * c].rearrange("p (k c) -> p k c", k=k)

        A = [rv(j, 0, 4, CB) for j in range(NB)]
        hglu = [rv(j, 2048, 4, 32 + CB) for j in range(NB)]
        cs = [rv(j, 4224, 4, CB) for j in range(NB)]
        C = [rv(j, 6272, 4, CB) for j in range(NB)]
        merged = [rv(j, 8320, 8, CB) for j in range(NB)]
        hid = [rv(j, 0, NFT, CB) for j in range(NB)]
        zp = sb("zp", [128, 4, CB], BF16)
        vn = sb("vn", [128, 4, CB], BF16)
        xc = sb("xc", [128, 4, CB], F32)
        xcb = sb("xcb", [128, 4, CB], BF16)
        sqc = sb("sqc", [128, 4, CB], BF16)
        zg = [sb("zg%d" % i, [128, 2 + CB], BF16) for i in range(2 * NB)]
        D31 = [sb("D31_%d" % i, [128, 16, 128], BF16) for i in range(2)]
        D3 = [sb("D3_%d" % i, [128, 3, 128], BF16) for i in range(4)]
        ws = [sb("ws%d" % i, [128, 4096], BF16) for i in range(NWS)]
        poolw = sb("poolw", [128, 4, 256], BF16)
        pvec = sb("pvec_sb", [128, depth * PV_L + 8], F32)
        cmat = sb("cmat_sb", [128, 14, 128], BF16)
        c32 = sb("c32_sb", [128, 2, 128], F32)
        wst32 = sb("wst32", [128, 4, 128], F32)
        wsT = sb("wsT", [128, 4, 128], BF16)
        gbc = sb("gbc", [128, 2, 512], F32)
        bsrow = sb("bsrow", [1, 512], BF16)
        epsr = sb("epsr", [128, 2], F32)
        halo_zp = [[sb("hzp%d_%d" % (l, j), [128, CB], BF16) for j in range(NB)] for l in range(depth)]
        halo_hg = [[sb("hhg%d_%d" % (l, j), [128, 4, 32], BF16) for j in range(NB)] for l in range(depth)]
        halo_zg = [[sb("hzg%d_%d" % (l, j), [128, NFT, 2], BF16) for j in range(NB)] for l in range(depth)]
        NT32 = 8
        t32 = [sb("t32_%d" % i, [128, CB], F32) for i in range(NT32)]
        NSM = 8
        sm = [sb("sm%d" % i, [128, 8], F32) for i in range(NSM)]
        ps = [es.enter_context(nc.psum_tensor("ps%d" % i, [128, CB], F32)) for i in range(8)]

        sems = {e: es.enter_context(nc.semaphore("s_" + e)) for e in ENGS}
        dnames = ["w%d%s" % (s, t) for s in range(NWS) for t in "abc"] + \
                 ["x%d" % j for j in range(NB)] + ["out", "pv", "cm", "c32", "sw", "gb", "bs", "pw"]
        dsems = {s: es.enter_context(nc.semaphore("d_" + s)) for s in dnames}
        block = es.enter_context(nc.Block())
        P = Prog()
        st = dict(bank=0, t32=0, sm=0, ws=0)

        def bank():
            b = st["bank"]
            st["bank"] = (b + 1) % 8
            return b

        def T32():
            i = st["t32"]
            st["t32"] = (i + 1) % NT32
            return i

        def SM():
            i = st["sm"]
            st["sm"] = (i + 1) % NSM
            return i

        def wslot():
            i = st["ws"]
            st["ws"] = (i + 1) % NWS
            return i

        def mm(b, out_ap, lhsT, rhs, start, stop, reads, writes=()):
            P.op("pe", lambda e: e.matmul(out_ap, lhsT=lhsT, rhs=rhs, start=start, stop=stop),
                 reads=reads, writes=[("ps", b)] + list(writes))

        def act(out, in_, func, reads, writes, bias=None, scale=None):
            kw = {}
            if bias is not None:
                kw["bias"] = bias
            if scale is not None:
                kw["scale"] = scale
            P.op("act", lambda e: e.activation(out=out, in_=in_, func=func, **kw), reads=reads, writes=writes)

        def tt(eng, out, in0, in1, op, reads, writes):
            P.op(eng, lambda e: e.tensor_tensor(out=out, in0=in0, in1=in1, op=op), reads=reads, writes=writes)

        def stt(out, in0, scalar, in1, op0, op1, reads, writes):
            P.op("dve", lambda e: e.scalar_tensor_tensor(out=out, in0=in0, scalar=scalar, in1=in1, op0=op0, op1=op1),
                 reads=reads, writes=writes)

        def dve(fn, reads, writes):
            P.op("dve", fn, reads=reads, writes=writes)

        def wload(slot, sub, dst, src):
            P.op("pool", lambda e: e.dma_start(out=dst, in_=src), writes=[("w", slot, sub)],
                 dma_slot="w%d%s" % (slot, sub))

        def wview(slot, off, kt, cols):
            return ws[slot][:, off:off + kt * cols].rearrange("p (k c) -> p k c", k=kt)

        def rows(ap2d):
            return ap2d.rearrange("(k p) c -> p k c", p=128)

        P.op("sp", lambda e: e.dma_start(out=pvec[:], in_=pvec_d), writes=["pvec"], dma_slot="pv")
        P.op("pool", lambda e: e.dma_start(out=cmat[:], in_=cmat_d), writes=["cmat"], dma_slot="cm")
        P.op("sp", lambda e: e.dma_start(out=c32[:], in_=c32_d), writes=["c32"], dma_slot="c32")
        ident = cmat[:, 0, :]
        ones = cmat[:, 1, :]
        ident32 = c32[:, 0, :]
        tril32 = c32[:, 1, :]
        ones_row = cmat[0:1, 1, :]
        dve(lambda e: e.memset(epsr[:, 0:1], 1e-6), [], ["eps"])
        dve(lambda e: e.memset(epsr[:, 1:2], 1e-5), [], ["eps"])
        JS = list(range(NB))

        for pi in range(n_blk):
            first = pi == 0
            last_in_seq = pi == n_blk - 1
            tok0 = [(j * n_blk + pi) * CB for j in JS]
            P.label = "xload"
            for j in JS:
                P.op("sp", lambda e, j=j, t0=tok0[j]: e.dma_start(out=xb[j][:], in_=rows(xT[:, t0:t0 + CB])),
                     writes=[("x", j, k) for k in range(8)], dma_slot="x%d" % j)

            def rms_stats(j):
                x = xb[j]
                for k in range(8):
                    act(hT[j][:, k, :], x[:, k, :], AF.Square, reads=[("x", j, k)], writes=[("h", j, k)])
                b = bank()
                for k in range(8):
                    mm(b, ps[b][:], ones, hT[j][:, k, :], k == 0, k == 7, reads=["cmat", ("h", j, k)])
                sd = T32()
                act(t32[sd][:], ps[b][:], AF.Ln, reads=["eps"], writes=[("ps", b), ("t", sd)],
                    bias=epsr[:, 0:1], scale=1.0 / D)
                rs = T32()
                act(t32[rs][:], t32[sd][:], AF.Exp, reads=[("t", sd)], writes=[("t", rs)], scale=-0.5)
                return rs

            def rms(goff):
                for j in JS:
                    rs = rms_stats(j)
                    for k in range(8):
                        stt(hT[j][:, k, :], xb[j][:, k, :], pvec[:, goff + k:goff + k + 1], t32[rs][:], ALU.mult, ALU.mult,
                            reads=[("x", j, k), ("t", rs), "pvec"], writes=[("h", j, k)])

            for l in range(depth):
                LP = l * PV_L
                P.label = "rms1"
                rms(LP + PV_NMIX)

                P.label = "pool"
                s = wslot()
                wload(s, "a", wview(s, 0, 8, 512), rows(w_in[l][:, 0:512]))
                wp = wview(s, 0, 8, 512)
                for j in JS:
                    for q in range(4):
                        b = bank()
                        for k in range(8):
                            mm(b, ps[b][:], hT[j][:, k, q * 128:(q + 1) * 128], wp[:, k, :], k == 0, k == 7,
                               reads=[("h", j, k), ("w", s, "a")])
                        act(zp[:, q, :], ps[b][:], AF.Copy, reads=[], writes=[("ps", b), ("zp", q)])
                    for g in range(4):
                        b = bank()
                        for q in range(4):
                            ft = first and q == 0
                            o = ps[b][:, q * 128:(q + 1) * 128]
                            mm(b, o, zp[:, q, g * 128:(g + 1) * 128], cmat[:, (6 if ft else 2) + g, :], True, ft,
                               reads=[("zp", q), "cmat"])
                            if not ft:
                                prev = halo_zp[l][j][:, g * 128:(g + 1) * 128] if q == 0 else zp[:, q - 1, g * 128:(g + 1) * 128]
                                mm(b, o, prev, cmat[:, 10 + g, :], False, True,
                                   reads=[("hzp", l, j) if q == 0 else ("zp", q - 1), "cmat"])
                        dve(lambda e, b=b, g=g, j=j: e.tensor_copy(out=A[j][:, g, :], in_=ps[b][:]),
                            [("R", j)], [("ps", b), ("A", j, g)])
                    if not last_in_seq:
                        dve(lambda e, l=l, j=j: e.tensor_copy(out=halo_zp[l][j][:], in_=zp[:, 3, :]),
                            [("zp", 3)], [("hzp", l, j)])

                P.label = "glu"
                cdw = LP + PV_CDW

                def build_d31(c):
                    for k in range(31):
                        col = cdw + c * 31 + k
                        dve(lambda e, k=k, col=col: e.tensor_scalar(
                            out=D31[k // 16][:, k % 16, :], in0=ident, scalar1=pvec[:, col:col + 1], scalar2=None, op0=ALU.mult),
                            ["cmat", "pvec"], [("D31", k // 16)])

                build_d31(0)
                sv = wslot()
                wload(sv, "a", wview(sv, 0, 8, 512), rows(w_in[l][:, 512:1024]))
                sg_ = wslot()
                wload(sg_, "a", wview(sg_, 0, 8, 512), rows(w_in[l][:, 1024:1536]))
                wv_ = wview(sv, 0, 8, 512)
                wg_ = wview(sg_, 0, 8, 512)
                for j in JS:
                    if first:
                        dve(lambda e, j=j: e.memset(hglu[j][:, :, 0:32], 0.0), [("R", j)], [("hgh", j)])
                    else:
                        dve(lambda e, l=l, j=j: e.tensor_copy(out=hglu[j][:, :, 0:32], in_=halo_hg[l][j][:]),
                            [("hhg", l, j), ("R", j)], [("hgh", j)])
                for c in range(4):
                    for j in JS:
                        bv = bank()
                        for k in range(8):
                            mm(bv, ps[bv][:], wv_[:, k, c * 128:(c + 1) * 128], hT[j][:, k, :], k == 0, k == 7,
                               reads=[("h", j, k), ("w", sv, "a")])
                        bg = bank()
                        for k in range(8):
                            mm(bg, ps[bg][:], wg_[:, k, c * 128:(c + 1) * 128], hT[j][:, k, :], k == 0, k == 7,
                               reads=[("h", j, k), ("w", sg_, "a")])
                        t = T32()
                        act(t32[t][:], ps[bg][:], AF.Sigmoid, reads=[], writes=[("ps", bg), ("t", t)])
                        tt("dve", hglu[j][:, c, 32:32 + CB], ps[bv][:], t32[t][:], ALU.mult,
                           reads=[("t", t), ("R", j)], writes=[("ps", bv), ("hg", j, c)])
                if not last_in_seq:
                    for j in JS:
                        dve(lambda e, l=l, j=j: e.tensor_copy(out=halo_hg[l][j][:], in_=hglu[j][:, :, CB:CB + 32]),
                            [("hg", j, c) for c in range(4)] + [("R", j)], [("hhg", l, j)])
                P.label = "dwconv"
                cdw = LP + PV_CDW
                swd = sgu_w[l].rearrange("h t s -> t h s")
                P.op("sp", lambda e, swd=swd: e.dma_start(out=wst32[:], in_=swd), writes=["wst32"], dma_slot="sw")
                gbd = sgb_d[l].rearrange("a p c -> p a c")
                P.op("sp", lambda e, gbd=gbd: e.dma_start(out=gbc[:], in_=gbd), writes=["gbc"], dma_slot="gb")
                P.op("pool", lambda e, l=l: e.dma_start(out=bsrow[:], in_=sbrow_d[l]), writes=["bsrow"], dma_slot="bs")
                su = wslot()
                wload(su, "a", wview(su, 0, 8, 512), rows(w_in[l][:, 1536:2048]))
                sv2 = wslot()
                wload(sv2, "a", wview(sv2, 0, 8, 512), rows(w_in[l][:, 2048:2560]))
                wu_ = wview(su, 0, 8, 512)
                wv2 = wview(sv2, 0, 8, 512)
                for h in range(4):
                    tt("dve", wst32[:, h, :], wst32[:, h, :], tril32, ALU.mult, reads=["wst32", "c32"], writes=["wst32"])

                def conv_ln(j):
                    b1 = bank()
                    for c in range(4):
                        mm(b1, ps[b1][:], ones, xcb[:, c, :], c == 0, c == 3, reads=["cmat", ("xcb", c)])
                    b2 = bank()
                    for c in range(4):
                        mm(b2, ps[b2][:], ones, sqc[:, c, :], c == 0, c == 3, reads=["cmat", ("sqc", c)])
                    tm = T32()
                    act(t32[tm][:], ps[b1][:], AF.Identity, reads=[], writes=[("ps", b1), ("t", tm)], scale=1.0 / 512)
                    tq = T32()
                    act(t32[tq][:], ps[b1][:], AF.Square, reads=[], writes=[("ps", b1), ("t", tq)], scale=1.0 / 512)
                    tv = T32()
                    stt(t32[tv][:], ps[b2][:], 1.0 / 512, t32[tq][:], ALU.mult, ALU.subtract,
                        reads=[("t", tq)], writes=[("ps", b2), ("t", tv)])
                    act(t32[tq][:], t32[tv][:], AF.Ln, reads=[("t", tv), "eps"], writes=[("t", tq)], bias=epsr[:, 1:2])
                    act(t32[tv][:], t32[tq][:], AF.Exp, reads=[("t", tq)], writes=[("t", tv)], scale=-0.5)
                    trs = tv
                    tt("dve", t32[tq][:], t32[tm][:], t32[trs][:], ALU.mult, reads=[("t", tm), ("t", trs)], writes=[("t", tq)])
                    tn = tq
                    for c in range(4):
                        t1 = T32()
                        tt("pool", t32[t1][:], xc[:, c, :], t32[trs][:], ALU.mult, reads=[("xc", c), ("t", trs)], writes=[("t", t1)])
                        tt("dve", t32[t1][:], t32[t1][:], t32[tn][:], ALU.subtract, reads=[("t", t1), ("t", tn)], writes=[("t", t1)])
                        gcol = pvec[:, LP + PV_CLG + c:LP + PV_CLG + c + 1]
                        bcol2 = pvec[:, LP + PV_CLB + c:LP + PV_CLB + c + 1]
                        act(cs[j][:, c, :], t32[t1][:], AF.Silu, reads=[("t", t1), "pvec", ("R", j)], writes=[("cs", j, c)],
                            bias=bcol2, scale=gcol)

                pending = None
                units = [(j, c) for j in JS for c in range(4)]
                for ui, (j, c) in enumerate(units):
                    if True:
                        b = bank()
                        for k in range(31):
                            mm(b, ps[b][:], D31[k // 16][:, k % 16, :], hglu[j][:, c, k + 2:k + 2 + CB], k == 0, k == 30,
                               reads=[("D31", k // 16), ("hg", j, c), ("hgh", j), ("R", j)])
                        if ui + 1 < len(units):
                            build_d31(units[ui + 1][1])
                        if pending is not None:
                            conv_ln(pending)
                            pending = None
                        bcol = pvec[:, LP + PV_CDB + c:LP + PV_CDB + c + 1]
                        act(xc[:, c, :], ps[b][:], AF.Identity, reads=["pvec"], writes=[("ps", b), ("xc", c)], bias=bcol)
                        act(xcb[:, c, :], ps[b][:], AF.Identity, reads=["pvec"], writes=[("ps", b), ("xcb", c)], bias=bcol)
                        act(sqc[:, c, :], ps[b][:], AF.Square, reads=["pvec"], writes=[("ps", b), ("sqc", c)], bias=bcol)
                        if c == 3:
                            pending = j

                P.label = "sgu_v"
                b = bank()
                for h in range(4):
                    P.op("pe", lambda e, b=b, h=h: e.transpose(out=ps[b][:, h * 128:(h + 1) * 128], in_=wst32[:, h, :], identity=ident32),
                         reads=["wst32", "c32"], writes=[("ps", b)])
                act(wsT[:].rearrange("p h t -> p (h t)"), ps[b][:], AF.Copy, reads=[], writes=[("ps", b), "wsT"])
                vnb = [vn, zp]
                vres = ["vn", "zp"]
                for j in JS:
                    tgs, s2s, s4s, s5s = [], [], [], []
                    for q in range(4):
                        b = bank()
                        for k in range(8):
                            mm(b, ps[b][:], hT[j][:, k, q * 128:(q + 1) * 128], wv2[:, k, :], k == 0, k == 7,
                               reads=[("h", j, k), ("w", sv2, "a")])
                        tg = T32()
                        tgs.append(tg)
                        act(t32[tg][:], ps[b][:], AF.Gelu_apprx_tanh, reads=[], writes=[("ps", b), ("t", tg)])
                    for q in range(4):
                        tg = tgs[q]
                        s1 = SM()
                        dve(lambda e, s1=s1, tg=tg: e.bn_stats(out=sm[s1][:, 0:6], in_=t32[tg][:]), [("t", tg)], [("sm", s1)])
                        dve(lambda e, s1=s1: e.bn_aggr(out=sm[s1][:, 6:8], in_=sm[s1][:, 0:6]), [("sm", s1)], [("sm", s1)])
                        s2s.append(s1)
                    for q in range(4):
                        s2 = s2s[q]
                        act(sm[s2][:, 0:1], sm[s2][:, 7:8], AF.Sqrt, reads=[("sm", s2), "eps"], writes=[("sm", s2)], bias=epsr[:, 1:2])
                    for q in range(4):
                        s2 = s2s[q]
                        dve(lambda e, s2=s2: e.reciprocal(out=sm[s2][:, 1:2], in_=sm[s2][:, 0:1]), [("sm", s2)], [("sm", s2)])
                        stt(sm[s2][:, 2:3], sm[s2][:, 6:7], -1.0, sm[s2][:, 1:2], ALU.mult, ALU.mult,
                            reads=[("sm", s2)], writes=[("sm", s2)])
                    for q in range(4):
                        tg, s2 = tgs[q], s2s[q]
                        act(t32[tg][:], t32[tg][:], AF.Identity, reads=[("t", tg), ("sm", s2)], writes=[("t", tg)],
                            bias=sm[s2][:, 2:3], scale=sm[s2][:, 1:2])
                    for q in range(4):
                        tg = tgs[q]
                        tt("dve", t32[tg][:], t32[tg][:], gbc[:, 0, :], ALU.mult, reads=[("t", tg), "gbc"], writes=[("t", tg)])
                        tt("dve", vnb[j][:, q, :], t32[tg][:], gbc[:, 1, :], ALU.add, reads=[("t", tg), "gbc"], writes=[(vres[j], q)])
                    if pending is not None:
                        conv_ln(pending)
                        pending = None
                P.label = "sgu_u"
                for c in range(4):
                    for j in JS:
                        b = bank()
                        for k in range(8):
                            mm(b, ps[b][:], wu_[:, k, c * 128:(c + 1) * 128], hT[j][:, k, :], k == 0, k == 7,
                               reads=[("h", j, k), ("w", su, "a")])
                        act(C[j][:, c, :], ps[b][:], AF.Gelu_apprx_tanh, reads=[("R", j)], writes=[("ps", b), ("C", j, c)])
                P.label = "sgu_mix"
                for j in JS:
                    for h in range(4):
                        b = bank()
                        for q in range(4):
                            o = ps[b][:, q * 128:(q + 1) * 128]
                            mm(b, o, vnb[j][:, q, h * 128:(h + 1) * 128], wsT[:, h, :], True, False, reads=[(vres[j], q), "wsT"])
                            mm(b, o, ones_row, bsrow[0:1, h * 128:(h + 1) * 128], False, True, reads=["cmat", "bsrow"])
                        tt("dve", C[j][:, h, :], ps[b][:], C[j][:, h, :], ALU.mult, reads=[("C", j, h), ("R", j)],
                           writes=[("ps", b), ("C", j, h)])

                P.label = "merge"
                P.op("pool", lambda e, l=l: e.dma_start(out=poolw[:], in_=pool_w[l].rearrange("g c d -> c g d")),
                     writes=["poolw"], dma_slot="pw")
                for mp in range(4):
                    sx = wslot()
                    wload(sx, "a", wview(sx, 0, 8, 256), rows(w_in[l][:, 2560 + mp * 256:2560 + (mp + 1) * 256]))
                    wload(sx, "b", wview(sx, 2048, 8, 256), rows(w_in[l][:, 3584 + mp * 256:3584 + (mp + 1) * 256]))
                    sy = wslot()
                    wload(sy, "a", wview(sy, 0, 8, 256), rows(w_in[l][:, 4608 + mp * 256:4608 + (mp + 1) * 256]))
                    wload(sy, "b", wview(sy, 2048, 4, 256), rows(conv_pw[l][:, mp * 256:(mp + 1) * 256]))
                    wload(sy, "c", wview(sy, 3072, 4, 256), rows(sgu_out[l][:, mp * 256:(mp + 1) * 256]))
                    G = [(wview(sx, 0, 8, 256), ("w", sx, "a")), (wview(sx, 2048, 8, 256), ("w", sx, "b")),
                         (wview(sy, 0, 8, 256), ("w", sy, "a"))]
                    Wcp = wview(sy, 2048, 4, 256)
                    Wso = wview(sy, 3072, 4, 256)
                    for mi in range(2):
                        m = 2 * mp + mi
                        msl = slice(mi * 128, (mi + 1) * 128)
                        for j in JS:
                            acc = None
                            for br in range(3):
                                bg = bank()
                                for k in range(8):
                                    mm(bg, ps[bg][:], G[br][0][:, k, msl], hT[j][:, k, :], k == 0, k == 7, reads=[("h", j, k), G[br][1]])
                                by = bank()
                                if br == 0:
                                    mm(by, ps[by][:], poolw[:, mp, msl], A[j][:, mp, :], True, True, reads=["poolw", ("A", j, mp), ("R", j)])
                                elif br == 1:
                                    for k in range(4):
                                        mm(by, ps[by][:], Wcp[:, k, msl], cs[j][:, k, :], k == 0, k == 3,
                                           reads=[("w", sy, "b"), ("cs", j, k), ("R", j)])
                                else:
                                    for k in range(4):
                                        mm(by, ps[by][:], Wso[:, k, msl], C[j][:, k, :], k == 0, k == 3,
                                           reads=[("w", sy, "c"), ("C", j, k), ("R", j)])
                                tsg = T32()
                                bcol = pvec[:, LP + PV_BGATE + br * 8 + m:LP + PV_BGATE + br * 8 + m + 1]
                                act(t32[tsg][:], ps[bg][:], AF.Sigmoid, reads=["pvec"], writes=[("ps", bg), ("t", tsg)], bias=bcol)
                                if br == 0:
                                    acc = tsg
                                    scol = pvec[:, LP + PV_PSCALE + m:LP + PV_PSCALE + m + 1]
                                    stt(t32[acc][:], ps[by][:], scol, t32[tsg][:], ALU.mult, ALU.mult,
                                        reads=[("t", tsg), "pvec"], writes=[("ps", by), ("t", acc)])
                                else:
                                    tt("dve", t32[tsg][:], ps[by][:], t32[tsg][:], ALU.mult, reads=[("t", tsg)],
                                       writes=[("ps", by), ("t", tsg)])
                                    if br == 1:
                                        tt("dve", t32[acc][:], t32[acc][:], t32[tsg][:], ALU.add,
                                           reads=[("t", acc), ("t", tsg)], writes=[("t", acc)])
                                    else:
                                        tt("dve", merged[j][:, m, :], t32[acc][:], t32[tsg][:], ALU.add,
                                           reads=[("t", acc), ("t", tsg), ("R", j)], writes=[("mg", j, m)])
                P.label = "wout"
                for ng in range(2):
                    so = wslot()
                    wload(so, "a", wview(so, 0, 8, 512), rows(w_out[l][:, ng * 512:(ng + 1) * 512]))
                    wo = wview(so, 0, 8, 512)
                    order = [(ni, j) for ni in range(4) for j in JS] if ng == 0 else [(ni, j) for j in JS for ni in range(4)]
                    for ni, j in order:
                        n = 4 * ng + ni
                        b = bank()
                        for m in range(8):
                            mm(b, ps[b][:], wo[:, m, ni * 128:(ni + 1) * 128], merged[j][:, m, :], m == 0, m == 7,
                               reads=[("w", so, "a"), ("mg", j, m), ("R", j)])
                        tt("dve", xb[j][:, n, :], ps[b][:], xb[j][:, n, :], ALU.add, reads=[("x", j, n)],
                           writes=[("ps", b), ("x", j, n)])

                P.label = "rms2"
                rms(LP + PV_NFFN)
                P.label = "up"
                fdw = LP + PV_FDW
                for gi in range(11):
                    sf = wslot()
                    wload(sf, "a", wview(sf, 0, 8, 256), rows(ffn_up[l][:, gi * 256:(gi + 1) * 256]))
                    wload(sf, "b", wview(sf, 2048, 8, 256), rows(ffn_up[l][:, DFF + gi * 256:DFF + (gi + 1) * 256]))
                    Wg = wview(sf, 0, 8, 256)
                    Wv = wview(sf, 2048, 8, 256)
                    for ci in range(2):
                        c = 2 * gi + ci
                        csl = slice(ci * 128, (ci + 1) * 128)
                        wc = [pvec[:, fdw + c * 3 + k:fdw + c * 3 + k + 1] for k in range(3)]
                        bcol = pvec[:, LP + PV_FDB + c:LP + PV_FDB + c + 1]
                        accs = []
                        for j in JS:
                            z = (c % 2) * NB + j
                            b = bank()
                            for k in range(8):
                                mm(b, ps[b][:], Wg[:, k, csl], hT[j][:, k, :], k == 0, k == 7, reads=[("h", j, k), ("w", sf, "a")])
                            if first:
                                dve(lambda e, z=z: e.memset(zg[z][:, 0:2], 0.0), [], [("zgh", z)])
                            else:
                                dve(lambda e, z=z, c=c, l=l, j=j: e.tensor_copy(out=zg[z][:, 0:2], in_=halo_zg[l][j][:, c, :]),
                                    [("hzg", l, j, c)], [("zgh", z)])
                            act(zg[z][:, 2:2 + CB], ps[b][:], AF.Copy, reads=[], writes=[("ps", b), ("zg", z)])
                            if not last_in_seq:
                                dve(lambda e, z=z, c=c, l=l, j=j: e.tensor_copy(out=halo_zg[l][j][:, c, :], in_=zg[z][:, CB:CB + 2]),
                                    [("zg", z)], [("hzg", l, j, c)])
                            accs.append(T32())
                        for k in range(3):
                            for j in JS:
                                z = (c % 2) * NB + j
                                acc = accs[j]
                                if k == 0:
                                    dve(lambda e, z=z, acc=acc, w0=wc[0], bcol=bcol: e.tensor_scalar(
                                        out=t32[acc][:], in0=zg[z][:, 0:CB], scalar1=w0, scalar2=bcol, op0=ALU.mult, op1=ALU.add),
                                        [("zg", z), ("zgh", z), "pvec"], [("t", acc)])
                                else:
                                    stt(t32[acc][:], zg[z][:, k:k + CB], wc[k], t32[acc][:], ALU.mult, ALU.add,
                                        reads=[("zg", z), ("zgh", z), ("t", acc), "pvec"], writes=[("t", acc)])
                        for j in JS:
                            acc = accs[j]
                            act(t32[acc][:], t32[acc][:], AF.Silu, reads=[("t", acc)], writes=[("t", acc)])
                        for j in JS:
                            acc = accs[j]
                            bv = bank()
                            for k in range(8):
                                mm(bv, ps[bv][:], Wv[:, k, csl], hT[j][:, k, :], k == 0, k == 7, reads=[("h", j, k), ("w", sf, "b")])
                            tt("dve", hid[j][:, c, :], ps[bv][:], t32[acc][:], ALU.mult, reads=[("t", acc)],
                               writes=[("ps", bv), ("hid", j, c), ("R", j)])
                P.label = "down"
                for mp in range(4):
                    sx = wslot()
                    wload(sx, "a", wview(sx, 0, 11, 256), rows(ffn_down[l][0:1408, mp * 256:(mp + 1) * 256]))
                    sy = wslot()
                    wload(sy, "a", wview(sy, 0, 11, 256), rows(ffn_down[l][1408:2816, mp * 256:(mp + 1) * 256]))
                    WX = wview(sx, 0, 11, 256)
                    WY = wview(sy, 0, 11, 256)
                    grp = [(j, mi, bank()) for j in JS for mi in range(2)]
                    for half, (W_, s_) in enumerate(((WX, sx), (WY, sy))):
                        for j, mi, b in grp:
                            n = 2 * mp + mi
                            msl = slice(mi * 128, (mi + 1) * 128)
                            for cc in range(11):
                                c = half * 11 + cc
                                mm(b, ps[b][:], W_[:, cc, msl], hid[j][:, c, :], c == 0, c == NFT - 1,
                                   reads=[("w", s_, "a"), ("hid", j, c)], writes=[("R", j)])
                            if half == 1:
                                tt("dve", xb[j][:, n, :], ps[b][:], xb[j][:, n, :], ALU.add, reads=[("x", j, n)],
                                   writes=[("ps", b), ("x", j, n)])

            P.label = "final"
            goff = depth * PV_L
            for j in JS:
                rs = rms_stats(j)
                for k in range(8):
                    to = T32()
                    stt(t32[to][:], xb[j][:, k, :], pvec[:, goff + k:goff + k + 1], t32[rs][:], ALU.mult, ALU.mult,
                        reads=[("x", j, k), ("t", rs), "pvec"], writes=[("t", to)])
                    P.op("sp", lambda e, j=j, k=k, to=to, t0=tok0[j]: e.dma_start(out=outT[k * 128:(k + 1) * 128, t0:t0 + CB], in_=t32[to][:]),
                         reads=[("t", to)], dma_slot="out")
        P.wait_for("sp", [("dma:out", n_blk * NB * 8 - 1)])
        P.emit(block, sems, dsems)
        nc._prog = P
    return nc


def _consts():
    cm = np.zeros((14, 128, 128), np.float32)
    cm[0] = np.eye(128, dtype=np.float32)
    cm[1] = 1.0
    s = np.arange(128)[:, None]
    t = np.arange(128)[None, :]
    for g, w in enumerate((2, 4, 8, 16)):
        band = ((t - s >= 0) & (t - s <= w - 1)).astype(np.float32)
        cm[2 + g] = band / w - np.eye(128, dtype=np.float32)
        cnt = np.minimum(t + 1, w).astype(np.float32)
        cm[6 + g] = band / cnt - np.eye(128, dtype=np.float32)
        cm[10 + g] = ((s - t) >= (129 - w)).astype(np.float32) / w
    c32 = np.zeros((2, 128, 128), np.float32)
    c32[0] = np.eye(128, dtype=np.float32)
    c32[1] = (t <= s).astype(np.float32)
    return np.ascontiguousarray(cm.transpose(1, 0, 2)), np.ascontiguousarray(c32.transpose(1, 0, 2))


def _pack_params(inp, depth):
    pv = np.zeros((128, depth * PV_L + 8), np.float32)

    def cols(v):
        return np.asarray(v, np.float32).reshape(-1, 128).T

    for l in range(depth):
        o = l * PV_L
        pv[:, o + PV_NMIX:o + PV_NMIX + 8] = cols(inp["norm_mix"][l])
        pv[:, o + PV_BGATE:o + PV_BGATE + 24] = cols(inp["b_gate"][l])
        pv[:, o + PV_PSCALE:o + PV_PSCALE + 8] = cols(inp["pool_scale"][l])
        cdw = np.asarray(inp["conv_dw_w"][l], np.float32)
        pv[:, o + PV_CDW:o + PV_CDW + 124] = cdw.reshape(31, 4, 128).transpose(2, 1, 0).reshape(128, 124)
        pv[:, o + PV_CDB:o + PV_CDB + 4] = cols(inp["conv_dw_b"][l])
        pv[:, o + PV_CLG:o + PV_CLG + 4] = cols(inp["conv_ln_g"][l])
        pv[:, o + PV_CLB:o + PV_CLB + 4] = cols(inp["conv_ln_b"][l])
        pv[:, o + PV_NFFN:o + PV_NFFN + 8] = cols(inp["norm_ffn"][l])
        fdw = np.asarray(inp["ffn_dw_w"][l], np.float32)
        pv[:, o + PV_FDW:o + PV_FDW + 66] = fdw.reshape(3, NFT, 128).transpose(2, 1, 0).reshape(128, 66)
        pv[:, o + PV_FDB:o + PV_FDB + NFT] = cols(inp["ffn_dw_b"][l])
    pv[:, depth * PV_L:depth * PV_L + 8] = cols(inp["norm_final"])
    return pv


def run_module(inp, n_cores, seqs_per_core, depth, trace=False):
    x = np.asarray(inp["x"], np.float32)
    B, S, _ = x.shape
    assert B == n_cores * seqs_per_core and S % CB == 0
    n_blk = S // CB
    nc = build_nc(seqs_per_core, n_blk, depth)
    cm, c32 = _consts()
    pv = _pack_params(inp, depth)
    sgb = np.stack([np.broadcast_to(np.asarray(inp["sgu_ln_g"], np.float32)[:depth, None, :], (depth, 128, 512)),
                    np.broadcast_to(np.asarray(inp["sgu_ln_b"], np.float32)[:depth, None, :], (depth, 128, 512))], axis=1)
    sgb = np.ascontiguousarray(sgb)
    sbrow = np.ascontiguousarray(np.asarray(inp["sgu_b"], np.float32)[:depth].reshape(depth, 1, 512))
    shared = {
        "w_in": np.ascontiguousarray(inp["w_in"][:depth], np.float32),
        "pool_w": np.ascontiguousarray(inp["pool_w"][:depth], np.float32),
        "conv_pw": np.ascontiguousarray(inp["conv_pw"][:depth], np.float32),
        "sgu_w": np.ascontiguousarray(inp["sgu_w"][:depth], np.float32),
        "sgu_out": np.ascontiguousarray(inp["sgu_out"][:depth], np.float32),
        "w_out": np.ascontiguousarray(inp["w_out"][:depth], np.float32),
        "ffn_up": np.ascontiguousarray(inp["ffn_up"][:depth], np.float32),
        "ffn_down": np.ascontiguousarray(inp["ffn_down"][:depth], np.float32),
        "pvec": pv, "sgb": sgb, "sbrow": sbrow, "cmat": cm, "c32": c32,
    }
    in_maps = []
    for c in range(n_cores):
        xs = x[c * seqs_per_core:(c + 1) * seqs_per_core].reshape(seqs_per_core * S, D)
        m = dict(shared)
        m["xT"] = np.ascontiguousarray(xs.T)
        in_maps.append(m)
    res = run_bass_kernel_spmd(nc, in_maps, core_ids=list(range(n_cores)), trace=trace)
    outs = []
    for c in range(n_cores):
        o = np.asarray(res.results[c]["outT"], np.float32).T.reshape(seqs_per_core, S, D)
        outs.append(o)
    out = np.concatenate(outs, axis=0)
    return out, res


def kernel(**inputs):
    out, _ = run_module(inputs, 8, 2, 4)
    return out.astype(np.float32)
```

```python
import numpy as np
from contextlib import ExitStack
import concourse.bass as bass
import concourse.mybir as mybir
from concourse.bass_utils import run_bass_kernel_spmd

F32 = mybir.dt.float32
BF16 = mybir.dt.bfloat16
AF = mybir.ActivationFunctionType
ALU = mybir.AluOpType

D = 1024
CB = 512
INC = 5632
DFF = 2816
NFT = 22
ENGS = ("pe", "act", "dve", "pool", "sp")

PV_NMIX, PV_BGATE, PV_PSCALE, PV_CDW, PV_CDB, PV_CLG, PV_CLB, PV_NFFN, PV_FDW, PV_FDB = (
    0, 8, 32, 40, 164, 168, 172, 176, 184, 250)
PV_L = 272


class Prog:
    def __init__(self):
        self.ops = {e: [] for e in ENGS}
        self.last_write = {}
        self.readers = {}
        self.known = {e: {} for e in ENGS}
        self.dma_count = {}
        self.snap = {}
        self.signal = set()
        self.label = ""
        self.labels = {e: [] for e in ENGS}

    def op(self, eng, fn, reads=(), writes=(), dma_slot=None):
        idx = len(self.ops[eng])
        self.labels[eng].append(self.label)
        deps = {}
        raw = {}
        for r in reads:
            lw = self.last_write.get(r)
            if lw is not None:
                if deps.get(lw[0], -1) < lw[1]:
                    deps[lw[0]] = lw[1]
                if raw.get(lw[0], -1) < lw[1]:
                    raw[lw[0]] = lw[1]
        for r in writes:
            lw = self.last_write.get(r)
            if lw is not None and deps.get(lw[0], -1) < lw[1]:
                deps[lw[0]] = lw[1]
            for a, i in self.readers.get(r, {}).items():
                if deps.get(a, -1) < i:
                    deps[a] = i
        if dma_slot is not None:
            n_prev = self.dma_count.get(dma_slot, 0)
            if n_prev > 0:
                a = "dma:%s" % dma_slot
                if deps.get(a, -1) < n_prev - 1:
                    deps[a] = n_prev - 1
        waits = []
        for a, i in deps.items():
            if a == eng:
                if eng in ("act", "dve", "pool") and dma_slot is None and raw.get(a, -1) >= idx - 1:
                    waits.append((a, raw[a]))
                continue
            if self.known[eng].get(a, -1) >= i:
                continue
            waits.append((a, i))
        k = self.known[eng]
        for a, i in waits:
            if not a.startswith("dma:"):
                self.signal.add((a, i))
            if a != eng:
                if k.get(a, -1) < i:
                    k[a] = i
                for a2, i2 in self.snap.get((a, i), {}).items():
                    if a2 != eng and k.get(a2, -1) < i2:
                        k[a2] = i2
        self.ops[eng].append((fn, waits, dma_slot))
        if dma_slot is not None:
            n = self.dma_count.get(dma_slot, 0)
            self.dma_count[dma_slot] = n + 1
            actor, aidx = "dma:%s" % dma_slot, n
        else:
            actor, aidx = eng, idx
        snap = dict(k)
        if dma_slot is None:
            snap[eng] = idx
        self.snap[(actor, aidx)] = snap
        for r in writes:
            self.last_write[r] = (actor, aidx)
            self.readers[r] = {}
        for r in reads:
            self.readers.setdefault(r, {})[actor] = aidx
        return (actor, aidx)

    def wait_for(self, eng, handles):
        waits = []
        for a, i in handles:
            if a == eng or self.known[eng].get(a, -1) >= i:
                continue
            waits.append((a, i))
            if not a.startswith("dma:"):
                self.signal.add((a, i))
            self.known[eng][a] = i
        self.ops[eng].append((None, waits, None))
        self.labels[eng].append(self.label)

    def emit(self, block, sems, dma_sems):
        cnt = {}
        for e in ENGS:
            c = 0
            arr = []
            for i in range(len(self.ops[e])):
                if (e, i) in self.signal:
                    c += 1
                arr.append(c)
            cnt[e] = arr
        self.sig_counts = {e: (cnt[e][-1] if cnt[e] else 0) for e in ENGS}
        self.names = {e: [] for e in ENGS}

        def run(e, engine):
            for i, (fn, waits, dslot) in enumerate(self.ops[e]):
                for a, ai in waits:
                    if a.startswith("dma:"):
                        engine.wait_ge(dma_sems[a[4:]], 16 * (ai + 1))
                    else:
                        engine.wait_ge(sems[a], cnt[a][ai])
                if fn is None:
                    continue
                inst = fn(engine)
                try:
                    self.names[e].append((i, inst.ins.name))
                except Exception:
                    pass
                if dslot is not None:
                    inst.then_inc(dma_sems[dslot], 16)
                elif (e, i) in self.signal:
                    inst.then_inc(sems[e], 1)

        @block.tensor
        def _(eng):
            run("pe", eng)

        @block.scalar
        def _(eng):
            run("act", eng)

        @block.vector
        def _(eng):
            run("dve", eng)

        @block.gpsimd
        def _(eng):
            run("pool", eng)

        @block.sync
        def _(eng):
            run("sp", eng)


NWS = 3


def build_nc(n_seq, n_blk, depth):
    NB = n_seq
    ntok = n_seq * n_blk * CB
    nc = bass.Bass("TRN2", target_bir_lowering=False)

    def din(name, shape):
        return nc.dram_tensor(name, list(shape), F32, kind="ExternalInput").ap()

    xT = din("xT", [D, ntok])
    outT = nc.dram_tensor("outT", [D, ntok], F32, kind="ExternalOutput").ap()
    w_in = din("w_in", [depth, D, INC])
    pool_w = din("pool_w", [depth, 4, 128, 256])
    conv_pw = din("conv_pw", [depth, 512, D])
    sgu_w = din("sgu_w", [depth, 4, 128, 128])
    sgu_out = din("sgu_out", [depth, 512, D])
    w_out = din("w_out", [depth, D, D])
    ffn_up = din("ffn_up", [depth, D, 2 * DFF])
    ffn_down = din("ffn_down", [depth, DFF, D])
    pvec_d = din("pvec", [128, depth * PV_L + 8])
    sgb_d = din("sgb", [depth, 2, 128, 512])
    sbrow_d = din("sbrow", [depth, 1, 512])
    cmat_d = din("cmat", [128, 14, 128])
    c32_d = din("c32", [128, 2, 128])

    with ExitStack() as es:
        def sb(name, shape, dt):
            return es.enter_context(nc.sbuf_tensor(name, list(shape), dt))

        RW = 12416
        xb = [sb("xb%d" % j, [128, 8, CB], F32) for j in range(NB)]
        hT = [sb("hT%d" % j, [128, 8, CB], BF16) for j in range(NB)]
        R = [sb("R%d" % j, [128, RW], BF16) for j in range(NB)]

        def rv(j, off, k, c):
            return R[j][:, off:off + k * c].rearrange("p (k c) -> p k c", k=k)

        A = [rv(j, 0, 4, CB) for j in range(NB)]
        hglu = [rv(j, 2048, 4, 32 + CB) for j in range(NB)]
        cs = [rv(j, 4224, 4, CB) for j in range(NB)]
        C = [rv(j, 6272, 4, CB) for j in range(NB)]
        merged = [rv(j, 8320, 8, CB) for j in range(NB)]
        hid = [rv(j, 0, NFT, CB) for j in range(NB)]
        zp = sb("zp", [128, 4, CB], BF16)
        vn = sb("vn", [128, 4, CB], BF16)
        xc = sb("xc", [128, 4, CB], F32)
        xcb = sb("xcb", [128, 4, CB], BF16)
        sqc = sb("sqc", [128, 4, CB], BF16)
        zg = [sb("zg%d" % i, [128, 2 + CB], BF16) for i in range(2 * NB)]
        D31 = [sb("D31_%d" % i, [128, 16, 128], BF16) for i in range(2)]
        lnt = [sb("lnt%d" % i, [128, CB], F32) for i in range(2)]
        ws = [sb("ws%d" % i, [128, 4096], BF16) for i in range(NWS)]
        poolw = sb("poolw", [128, 4, 256], BF16)
        pvec = sb("pvec_sb", [128, depth * PV_L + 8], F32)
        cmat = sb("cmat_sb", [128, 14, 128], BF16)
        c32 = sb("c32_sb", [128, 2, 128], F32)
        wst32 = sb("wst32", [128, 4, 128], F32)
        wsT = sb("wsT", [128, 4, 128], BF16)
        gbc = sb("gbc", [128, 2, 512], F32)
        bsrow = sb("bsrow", [1, 512], BF16)
        epsr = sb("epsr", [128, 2], F32)
        halo_zp = [[sb("hzp%d_%d" % (l, j), [128, CB], BF16) for j in range(NB)] for l in range(depth)]
        halo_hg = [[sb("hhg%d_%d" % (l, j), [128, 4, 32], BF16) for j in range(NB)] for l in range(depth)]
        halo_zg = [[sb("hzg%d_%d" % (l, j), [128, NFT, 2], BF16) for j in range(NB)] for l in range(depth)]
        NT32 = 8
        t32 = [sb("t32_%d" % i, [128, CB], F32) for i in range(NT32)]
        NSM = 8
        sm = [sb("sm%d" % i, [128, 8], F32) for i in range(NSM)]
        ps = [es.enter_context(nc.psum_tensor("ps%d" % i, [128, CB], F32)) for i in range(8)]

        sems = {e: es.enter_context(nc.semaphore("s_" + e)) for e in ENGS}
        dnames = ["w%d%s" % (s, t) for s in range(NWS) for t in "abc"] + \
                 ["x%d" % j for j in range(NB)] + ["out", "pv", "cm", "c32", "sw", "gb", "bs", "pw"]
        dsems = {s: es.enter_context(nc.semaphore("d_" + s)) for s in dnames}
        block = es.enter_context(nc.Block())
        P = Prog()
        st = dict(bank=0, t32=0, sm=0, ws=0)

        def bank():
            b = st["bank"]
            st["bank"] = (b + 1) % 8
            return b

        def T32():
            i = st["t32"]
            st["t32"] = (i + 1) % NT32
            return i

        def SM():
            i = st["sm"]
            st["sm"] = (i + 1) % NSM
            return i

        def wslot():
            i = st["ws"]
            st["ws"] = (i + 1) % NWS
            return i

        def mm(b, out_ap, lhsT, rhs, start, stop, reads, writes=()):
            P.op("pe", lambda e: e.matmul(out_ap, lhsT=lhsT, rhs=rhs, start=start, stop=stop),
                 reads=reads, writes=[("ps", b)] + list(writes))

        def act(out, in_, func, reads, writes, bias=None, scale=None):
            kw = {}
            if bias is not None:
                kw["bias"] = bias
            if scale is not None:
                kw["scale"] = scale
            P.op("act", lambda e: e.activation(out=out, in_=in_, func=func, **kw), reads=reads, writes=writes)

        def tt(eng, out, in0, in1, op, reads, writes):
            P.op(eng, lambda e: e.tensor_tensor(out=out, in0=in0, in1=in1, op=op), reads=reads, writes=writes)

        def stt(out, in0, scalar, in1, op0, op1, reads, writes):
            P.op("dve", lambda e: e.scalar_tensor_tensor(out=out, in0=in0, scalar=scalar, in1=in1, op0=op0, op1=op1),
                 reads=reads, writes=writes)

        def dve(fn, reads, writes):
            P.op("dve", fn, reads=reads, writes=writes)

        def wload(slot, sub, dst, src):
            P.op("pool", lambda e: e.dma_start(out=dst, in_=src), writes=[("w", slot, sub)],
                 dma_slot="w%d%s" % (slot, sub))

        def wview(slot, off, kt, cols):
            return ws[slot][:, off:off + kt * cols].rearrange("p (k c) -> p k c", k=kt)

        def rows(ap2d):
            return ap2d.rearrange("(k p) c -> p k c", p=128)

        P.op("sp", lambda e: e.dma_start(out=pvec[:], in_=pvec_d), writes=["pvec"], dma_slot="pv")
        P.op("pool", lambda e: e.dma_start(out=cmat[:], in_=cmat_d), writes=["cmat"], dma_slot="cm")
        P.op("sp", lambda e: e.dma_start(out=c32[:], in_=c32_d), writes=["c32"], dma_slot="c32")
        ident = cmat[:, 0, :]
        ones = cmat[:, 1, :]
        ident32 = c32[:, 0, :]
        tril32 = c32[:, 1, :]
        ones_row = cmat[0:1, 1, :]
        dve(lambda e: e.memset(epsr[:, 0:1], 1e-6), [], ["eps"])
        dve(lambda e: e.memset(epsr[:, 1:2], 1e-5), [], ["eps"])
        JS = list(range(NB))

        for pi in range(n_blk):
            first = pi == 0
            last_in_seq = pi == n_blk - 1
            tok0 = [(j * n_blk + pi) * CB for j in JS]
            P.label = "xload"
            for j in JS:
                P.op("sp", lambda e, j=j, t0=tok0[j]: e.dma_start(out=xb[j][:], in_=rows(xT[:, t0:t0 + CB])),
                     writes=[("x", j, k) for k in range(8)], dma_slot="x%d" % j)

            def rms_stats(j):
                x = xb[j]
                for k in range(8):
                    act(hT[j][:, k, :], x[:, k, :], AF.Square, reads=[("x", j, k)], writes=[("h", j, k)])
                b = bank()
                for k in range(8):
                    mm(b, ps[b][:], ones, hT[j][:, k, :], k == 0, k == 7, reads=["cmat", ("h", j, k)])
                sd = T32()
                act(t32[sd][:], ps[b][:], AF.Ln, reads=["eps"], writes=[("ps", b), ("t", sd)],
                    bias=epsr[:, 0:1], scale=1.0 / D)
                rs = T32()
                act(t32[rs][:], t32[sd][:], AF.Exp, reads=[("t", sd)], writes=[("t", rs)], scale=-0.5)
                return rs

            def rms(goff):
                for j in JS:
                    rs = rms_stats(j)
                    for k in range(8):
                        stt(hT[j][:, k, :], xb[j][:, k, :], pvec[:, goff + k:goff + k + 1], t32[rs][:], ALU.mult, ALU.mult,
                            reads=[("x", j, k), ("t", rs), "pvec"], writes=[("h", j, k)])

            for l in range(depth):
                LP = l * PV_L
                P.label = "rms1"
                rms(LP + PV_NMIX)

                P.label = "pool"
                s = wslot()
                wload(s, "a", wview(s, 0, 8, 512), rows(w_in[l][:, 0:512]))
                wp = wview(s, 0, 8, 512)
                for j in JS:
                    for q in range(4):
                        b = bank()
                        for k in range(8):
                            mm(b, ps[b][:], hT[j][:, k, q * 128:(q + 1) * 128], wp[:, k, :], k == 0, k == 7,
                               reads=[("h", j, k), ("w", s, "a")])
                        act(zp[:, q, :], ps[b][:], AF.Copy, reads=[], writes=[("ps", b), ("zp", q)])
                    for g in range(4):
                        b = bank()
                        for q in range(4):
                            ft = first and q == 0
                            o = ps[b][:, q * 128:(q + 1) * 128]
                            mm(b, o, zp[:, q, g * 128:(g + 1) * 128], cmat[:, (6 if ft else 2) + g, :], True, ft,
                               reads=[("zp", q), "cmat"])
                            if not ft:
                                prev = halo_zp[l][j][:, g * 128:(g + 1) * 128] if q == 0 else zp[:, q - 1, g * 128:(g + 1) * 128]
                                mm(b, o, prev, cmat[:, 10 + g, :], False, True,
                                   reads=[("hzp", l, j) if q == 0 else ("zp", q - 1), "cmat"])
                        dve(lambda e, b=b, g=g, j=j: e.tensor_copy(out=A[j][:, g, :], in_=ps[b][:]),
                            [("R", j)], [("ps", b), ("A", j, g)])
                    if not last_in_seq:
                        dve(lambda e, l=l, j=j: e.tensor_copy(out=halo_zp[l][j][:], in_=zp[:, 3, :]),
                            [("zp", 3)], [("hzp", l, j)])

                P.label = "glu"
                cdw = LP + PV_CDW

                def build_d31(c):
                    for k in range(31):
                        col = cdw + c * 31 + k
                        dve(lambda e, k=k, col=col: e.tensor_scalar(
                            out=D31[k // 16][:, k % 16, :], in0=ident, scalar1=pvec[:, col:col + 1], scalar2=None, op0=ALU.mult),
                            ["cmat", "pvec"], [("D31", k // 16)])

                build_d31(0)
                sv = wslot()
                wload(sv, "a", wview(sv, 0, 8, 512), rows(w_in[l][:, 512:1024]))
                sg_ = wslot()
                wload(sg_, "a", wview(sg_, 0, 8, 512), rows(w_in[l][:, 1024:1536]))
                wv_ = wview(sv, 0, 8, 512)
                wg_ = wview(sg_, 0, 8, 512)
                for j in JS:
                    if first:
                        dve(lambda e, j=j: e.memset(hglu[j][:, :, 0:32], 0.0), [("R", j)], [("hgh", j)])
                    else:
                        dve(lambda e, l=l, j=j: e.tensor_copy(out=hglu[j][:, :, 0:32], in_=halo_hg[l][j][:]),
                            [("hhg", l, j), ("R", j)], [("hgh", j)])
                for c in range(4):
                    for j in JS:
                        bv = bank()
                        for k in range(8):
                            mm(bv, ps[bv][:], wv_[:, k, c * 128:(c + 1) * 128], hT[j][:, k, :], k == 0, k == 7,
                               reads=[("h", j, k), ("w", sv, "a")])
                        bg = bank()
                        for k in range(8):
                            mm(bg, ps[bg][:], wg_[:, k, c * 128:(c + 1) * 128], hT[j][:, k, :], k == 0, k == 7,
                               reads=[("h", j, k), ("w", sg_, "a")])
                        t = T32()
                        act(t32[t][:], ps[bg][:], AF.Sigmoid, reads=[], writes=[("ps", bg), ("t", t)])
                        tt("dve", hglu[j][:, c, 32:32 + CB], ps[bv][:], t32[t][:], ALU.mult,
                           reads=[("t", t), ("R", j)], writes=[("ps", bv), ("hg", j, c)])
                if not last_in_seq:
                    for j in JS:
                        dve(lambda e, l=l, j=j: e.tensor_copy(out=halo_hg[l][j][:], in_=hglu[j][:, :, CB:CB + 32]),
                            [("hg", j, c) for c in range(4)] + [("R", j)], [("hhg", l, j)])
                P.label = "dwconv"
                cdw = LP + PV_CDW
                swd = sgu_w[l].rearrange("h t s -> t h s")
                P.op("sp", lambda e, swd=swd: e.dma_start(out=wst32[:], in_=swd), writes=["wst32"], dma_slot="sw")
                gbd = sgb_d[l].rearrange("a p c -> p a c")
                P.op("sp", lambda e, gbd=gbd: e.dma_start(out=gbc[:], in_=gbd), writes=["gbc"], dma_slot="gb")
                P.op("pool", lambda e, l=l: e.dma_start(out=bsrow[:], in_=sbrow_d[l]), writes=["bsrow"], dma_slot="bs")
                su = wslot()
                wload(su, "a", wview(su, 0, 8, 512), rows(w_in[l][:, 1536:2048]))
                sv2 = wslot()
                wload(sv2, "a", wview(sv2, 0, 8, 512), rows(w_in[l][:, 2048:2560]))
                wu_ = wview(su, 0, 8, 512)
                wv2 = wview(sv2, 0, 8, 512)
                for h in range(4):
                    tt("dve", wst32[:, h, :], wst32[:, h, :], tril32, ALU.mult, reads=["wst32", "c32"], writes=["wst32"])

                def conv_ln(j):
                    b1 = bank()
                    for c in range(4):
                        mm(b1, ps[b1][:], ones, xcb[:, c, :], c == 0, c == 3, reads=["cmat", ("xcb", c)])
                    b2 = bank()
                    for c in range(4):
                        mm(b2, ps[b2][:], ones, sqc[:, c, :], c == 0, c == 3, reads=["cmat", ("sqc", c)])
                    tm = T32()
                    act(t32[tm][:], ps[b1][:], AF.Identity, reads=[], writes=[("ps", b1), ("t", tm)], scale=1.0 / 512)
                    tq = 0
                    act(lnt[tq][:], ps[b1][:], AF.Square, reads=[], writes=[("ps", b1), ("lnt", tq)], scale=1.0 / 512)
                    tv = 1
                    stt(lnt[tv][:], ps[b2][:], 1.0 / 512, lnt[tq][:], ALU.mult, ALU.subtract,
                        reads=[("lnt", tq)], writes=[("ps", b2), ("lnt", tv)])
                    act(lnt[tq][:], lnt[tv][:], AF.Ln, reads=[("lnt", tv), "eps"], writes=[("lnt", tq)], bias=epsr[:, 1:2])
                    act(lnt[tv][:], lnt[tq][:], AF.Exp, reads=[("lnt", tq)], writes=[("lnt", tv)], scale=-0.5)
                    trs = tv
                    tt("dve", lnt[tq][:], t32[tm][:], lnt[trs][:], ALU.mult, reads=[("t", tm), ("lnt", trs)], writes=[("lnt", tq)])
                    tn = tq

                    def part_b():
                        t1s = []
                        for c in range(4):
                            t1 = T32()
                            t1s.append(t1)
                            tt("dve", t32[t1][:], xc[:, c, :], lnt[trs][:], ALU.mult, reads=[("xc", c), ("lnt", trs)], writes=[("t", t1)])
                        for c in range(4):
                            t1 = t1s[c]
                            tt("dve", t32[t1][:], t32[t1][:], lnt[tn][:], ALU.subtract, reads=[("t", t1), ("lnt", tn)], writes=[("t", t1)])
                        for c in range(4):
                            t1 = t1s[c]
                            gcol = pvec[:, LP + PV_CLG + c:LP + PV_CLG + c + 1]
                            bcol2 = pvec[:, LP + PV_CLB + c:LP + PV_CLB + c + 1]
                            act(cs[j][:, c, :], t32[t1][:], AF.Silu, reads=[("t", t1), "pvec", ("R", j)], writes=[("cs", j, c)],
                                bias=bcol2, scale=gcol)
                    return part_b

                pending = None
                units = [(j, c) for j in JS for c in range(4)]
                for ui, (j, c) in enumerate(units):
                    if True:
                        b = bank()
                        for k in range(31):
                            mm(b, ps[b][:], D31[k // 16][:, k % 16, :], hglu[j][:, c, k + 2:k + 2 + CB], k == 0, k == 30,
                               reads=[("D31", k // 16), ("hg", j, c), ("hgh", j), ("R", j)])
                        if ui + 1 < len(units):
                            build_d31(units[ui + 1][1])
                        if pending is not None:
                            pb = conv_ln(pending)
                            pending = None
                            pb()
                        bcol = pvec[:, LP + PV_CDB + c:LP + PV_CDB + c + 1]
                        act(xc[:, c, :], ps[b][:], AF.Identity, reads=["pvec"], writes=[("ps", b), ("xc", c)], bias=bcol)
                        act(xcb[:, c, :], ps[b][:], AF.Identity, reads=["pvec"], writes=[("ps", b), ("xcb", c)], bias=bcol)
                        act(sqc[:, c, :], ps[b][:], AF.Square, reads=["pvec"], writes=[("ps", b), ("sqc", c)], bias=bcol)
                        if c == 3:
                            pending = j

                P.label = "sgu_v"
                b = bank()
                for h in range(4):
                    P.op("pe", lambda e, b=b, h=h: e.transpose(out=ps[b][:, h * 128:(h + 1) * 128], in_=wst32[:, h, :], identity=ident32),
                         reads=["wst32", "c32"], writes=[("ps", b)])
                act(wsT[:].rearrange("p h t -> p (h t)"), ps[b][:], AF.Copy, reads=[], writes=[("ps", b), "wsT"])
                vnb = [vn, zp]
                vres = ["vn", "zp"]
                pbl = None
                for j in JS:
                    tgs, s2s, s4s, s5s = [], [], [], []
                    for q in range(4):
                        b = bank()
                        for k in range(8):
                            mm(b, ps[b][:], hT[j][:, k, q * 128:(q + 1) * 128], wv2[:, k, :], k == 0, k == 7,
                               reads=[("h", j, k), ("w", sv2, "a")])
                        tg = T32()
                        tgs.append(tg)
                        act(t32[tg][:], ps[b][:], AF.Gelu_apprx_tanh, reads=[], writes=[("ps", b), ("t", tg)])
                    if pending is not None and (j == 0 or NB == 1):
                        pbl = conv_ln(pending)
                        pending = None
                        if NB == 1:
                            pbl()
                            pbl = None
                    elif j == NB - 1 and pbl is not None:
                        pbl()
                        pbl = None
                    for q in range(4):
                        tg = tgs[q]
                        s1 = SM()
                        dve(lambda e, s1=s1, tg=tg: e.bn_stats(out=sm[s1][:, 0:6], in_=t32[tg][:]), [("t", tg)], [("sm", s1)])
                        dve(lambda e, s1=s1: e.bn_aggr(out=sm[s1][:, 6:8], in_=sm[s1][:, 0:6]), [("sm", s1)], [("sm", s1)])
                        s2s.append(s1)
                    for q in range(4):
                        s2 = s2s[q]
                        act(sm[s2][:, 0:1], sm[s2][:, 7:8], AF.Sqrt, reads=[("sm", s2), "eps"], writes=[("sm", s2)], bias=epsr[:, 1:2])
                    for q in range(4):
                        s2 = s2s[q]
                        dve(lambda e, s2=s2: e.reciprocal(out=sm[s2][:, 1:2], in_=sm[s2][:, 0:1]), [("sm", s2)], [("sm", s2)])
                        stt(sm[s2][:, 2:3], sm[s2][:, 6:7], -1.0, sm[s2][:, 1:2], ALU.mult, ALU.mult,
                            reads=[("sm", s2)], writes=[("sm", s2)])
                    for q in range(4):
                        tg, s2 = tgs[q], s2s[q]
                        act(t32[tg][:], t32[tg][:], AF.Identity, reads=[("t", tg), ("sm", s2)], writes=[("t", tg)],
                            bias=sm[s2][:, 2:3], scale=sm[s2][:, 1:2])
                    for q in range(4):
                        tg = tgs[q]
                        tt("dve", t32[tg][:], t32[tg][:], gbc[:, 0, :], ALU.mult, reads=[("t", tg), "gbc"], writes=[("t", tg)])
                        tt("dve", vnb[j][:, q, :], t32[tg][:], gbc[:, 1, :], ALU.add, reads=[("t", tg), "gbc"], writes=[(vres[j], q)])
                P.label = "sgu_u"
                for c in range(4):
                    for j in JS:
                        b = bank()
                        for k in range(8):
                            mm(b, ps[b][:], wu_[:, k, c * 128:(c + 1) * 128], hT[j][:, k, :], k == 0, k == 7,
                               reads=[("h", j, k), ("w", su, "a")])
                        act(C[j][:, c, :], ps[b][:], AF.Gelu_apprx_tanh, reads=[("R", j)], writes=[("ps", b), ("C", j, c)])
                P.label = "sgu_mix"
                for j in JS:
                    for h in range(4):
                        b = bank()
                        for q in range(4):
                            o = ps[b][:, q * 128:(q + 1) * 128]
                            mm(b, o, vnb[j][:, q, h * 128:(h + 1) * 128], wsT[:, h, :], True, False, reads=[(vres[j], q), "wsT"])
                            mm(b, o, ones_row, bsrow[0:1, h * 128:(h + 1) * 128], False, True, reads=["cmat", "bsrow"])
                        tt("dve", C[j][:, h, :], ps[b][:], C[j][:, h, :], ALU.mult, reads=[("C", j, h), ("R", j)],
                           writes=[("ps", b), ("C", j, h)])

                P.label = "merge"
                P.op("pool", lambda e, l=l: e.dma_start(out=poolw[:], in_=pool_w[l].rearrange("g c d -> c g d")),
                     writes=["poolw"], dma_slot="pw")
                for mp in range(4):
                    sx = wslot()
                    wload(sx, "a", wview(sx, 0, 8, 256), rows(w_in[l][:, 2560 + mp * 256:2560 + (mp + 1) * 256]))
                    wload(sx, "b", wview(sx, 2048, 8, 256), rows(w_in[l][:, 3584 + mp * 256:3584 + (mp + 1) * 256]))
                    sy = wslot()
                    wload(sy, "a", wview(sy, 0, 8, 256), rows(w_in[l][:, 4608 + mp * 256:4608 + (mp + 1) * 256]))
                    wload(sy, "b", wview(sy, 2048, 4, 256), rows(conv_pw[l][:, mp * 256:(mp + 1) * 256]))
                    wload(sy, "c", wview(sy, 3072, 4, 256), rows(sgu_out[l][:, mp * 256:(mp + 1) * 256]))
                    G = [(wview(sx, 0, 8, 256), ("w", sx, "a")), (wview(sx, 2048, 8, 256), ("w", sx, "b")),
                         (wview(sy, 0, 8, 256), ("w", sy, "a"))]
                    Wcp = wview(sy, 2048, 4, 256)
                    Wso = wview(sy, 3072, 4, 256)
                    for mi in range(2):
                        m = 2 * mp + mi
                        msl = slice(mi * 128, (mi + 1) * 128)
                        for j in JS:
                            acc = None
                            for br in range(3):
                                bg = bank()
                                for k in range(8):
                                    mm(bg, ps[bg][:], G[br][0][:, k, msl], hT[j][:, k, :], k == 0, k == 7, reads=[("h", j, k), G[br][1]])
                                by = bank()
                                if br == 0:
                                    mm(by, ps[by][:], poolw[:, mp, msl], A[j][:, mp, :], True, True, reads=["poolw", ("A", j, mp), ("R", j)])
                                elif br == 1:
                                    for k in range(4):
                                        mm(by, ps[by][:], Wcp[:, k, msl], cs[j][:, k, :], k == 0, k == 3,
                                           reads=[("w", sy, "b"), ("cs", j, k), ("R", j)])
                                else:
                                    for k in range(4):
                                        mm(by, ps[by][:], Wso[:, k, msl], C[j][:, k, :], k == 0, k == 3,
                                           reads=[("w", sy, "c"), ("C", j, k), ("R", j)])
                                tsg = T32()
                                bcol = pvec[:, LP + PV_BGATE + br * 8 + m:LP + PV_BGATE + br * 8 + m + 1]
                                act(t32[tsg][:], ps[bg][:], AF.Sigmoid, reads=["pvec"], writes=[("ps", bg), ("t", tsg)], bias=bcol)
                                if br == 0:
                                    acc = tsg
                                    scol = pvec[:, LP + PV_PSCALE + m:LP + PV_PSCALE + m + 1]
                                    stt(t32[acc][:], ps[by][:], scol, t32[tsg][:], ALU.mult, ALU.mult,
                                        reads=[("t", tsg), "pvec"], writes=[("ps", by), ("t", acc)])
                                else:
                                    tt("dve", t32[tsg][:], ps[by][:], t32[tsg][:], ALU.mult, reads=[("t", tsg)],
                                       writes=[("ps", by), ("t", tsg)])
                                    if br == 1:
                                        tt("dve", t32[acc][:], t32[acc][:], t32[tsg][:], ALU.add,
                                           reads=[("t", acc), ("t", tsg)], writes=[("t", acc)])
                                    else:
                                        tt("dve", merged[j][:, m, :], t32[acc][:], t32[tsg][:], ALU.add,
                                           reads=[("t", acc), ("t", tsg), ("R", j)], writes=[("mg", j, m)])
                P.label = "wout"
                for ng in range(2):
                    so = wslot()
                    wload(so, "a", wview(so, 0, 8, 512), rows(w_out[l][:, ng * 512:(ng + 1) * 512]))
                    wo = wview(so, 0, 8, 512)
                    order = [(ni, j) for ni in range(4) for j in JS] if ng == 0 else [(ni, j) for j in JS for ni in range(4)]
                    for ni, j in order:
                        n = 4 * ng + ni
                        b = bank()
                        for m in range(8):
                            mm(b, ps[b][:], wo[:, m, ni * 128:(ni + 1) * 128], merged[j][:, m, :], m == 0, m == 7,
                               reads=[("w", so, "a"), ("mg", j, m), ("R", j)])
                        tt("dve", xb[j][:, n, :], ps[b][:], xb[j][:, n, :], ALU.add, reads=[("x", j, n)],
                           writes=[("ps", b), ("x", j, n)])

                P.label = "rms2"
                rms(LP + PV_NFFN)
                P.label = "up"
                fdw = LP + PV_FDW
                for gi in range(11):
                    sf = wslot()
                    wload(sf, "a", wview(sf, 0, 8, 256), rows(ffn_up[l][:, gi * 256:(gi + 1) * 256]))
                    wload(sf, "b", wview(sf, 2048, 8, 256), rows(ffn_up[l][:, DFF + gi * 256:DFF + (gi + 1) * 256]))
                    Wg = wview(sf, 0, 8, 256)
                    Wv = wview(sf, 2048, 8, 256)
                    for ci in range(2):
                        c = 2 * gi + ci
                        csl = slice(ci * 128, (ci + 1) * 128)
                        wc = [pvec[:, fdw + c * 3 + k:fdw + c * 3 + k + 1] for k in range(3)]
                        bcol = pvec[:, LP + PV_FDB + c:LP + PV_FDB + c + 1]
                        accs = []
                        for j in JS:
                            z = (c % 2) * NB + j
                            b = bank()
                            for k in range(8):
                                mm(b, ps[b][:], Wg[:, k, csl], hT[j][:, k, :], k == 0, k == 7, reads=[("h", j, k), ("w", sf, "a")])
                            if first:
                                dve(lambda e, z=z: e.memset(zg[z][:, 0:2], 0.0), [], [("zgh", z)])
                            else:
                                dve(lambda e, z=z, c=c, l=l, j=j: e.tensor_copy(out=zg[z][:, 0:2], in_=halo_zg[l][j][:, c, :]),
                                    [("hzg", l, j, c)], [("zgh", z)])
                            act(zg[z][:, 2:2 + CB], ps[b][:], AF.Copy, reads=[], writes=[("ps", b), ("zg", z)])
                            if not last_in_seq:
                                dve(lambda e, z=z, c=c, l=l, j=j: e.tensor_copy(out=halo_zg[l][j][:, c, :], in_=zg[z][:, CB:CB + 2]),
                                    [("zg", z)], [("hzg", l, j, c)])
                            accs.append(T32())
                        for k in range(3):
                            for j in JS:
                                z = (c % 2) * NB + j
                                acc = accs[j]
                                if k == 0:
                                    dve(lambda e, z=z, acc=acc, w0=wc[0], bcol=bcol: e.tensor_scalar(
                                        out=t32[acc][:], in0=zg[z][:, 0:CB], scalar1=w0, scalar2=bcol, op0=ALU.mult, op1=ALU.add),
                                        [("zg", z), ("zgh", z), "pvec"], [("t", acc)])
                                else:
                                    stt(t32[acc][:], zg[z][:, k:k + CB], wc[k], t32[acc][:], ALU.mult, ALU.add,
                                        reads=[("zg", z), ("zgh", z), ("t", acc), "pvec"], writes=[("t", acc)])
                        for j in JS:
                            acc = accs[j]
                            act(t32[acc][:], t32[acc][:], AF.Silu, reads=[("t", acc)], writes=[("t", acc)])
                        for j in JS:
                            acc = accs[j]
                            bv = bank()
                            for k in range(8):
                                mm(bv, ps[bv][:], Wv[:, k, csl], hT[j][:, k, :], k == 0, k == 7, reads=[("h", j, k), ("w", sf, "b")])
                            tt("dve", hid[j][:, c, :], ps[bv][:], t32[acc][:], ALU.mult, reads=[("t", acc)],
                               writes=[("ps", bv), ("hid", j, c), ("R", j)])
                P.label = "down"
                for mp in range(4):
                    sx = wslot()
                    wload(sx, "a", wview(sx, 0, 11, 256), rows(ffn_down[l][0:1408, mp * 256:(mp + 1) * 256]))
                    sy = wslot()
                    wload(sy, "a", wview(sy, 0, 11, 256), rows(ffn_down[l][1408:2816, mp * 256:(mp + 1) * 256]))
                    WX = wview(sx, 0, 11, 256)
                    WY = wview(sy, 0, 11, 256)
                    grp = [(j, mi, bank()) for j in JS for mi in range(2)]
                    for half, (W_, s_) in enumerate(((WX, sx), (WY, sy))):
                        for j, mi, b in grp:
                            n = 2 * mp + mi
                            msl = slice(mi * 128, (mi + 1) * 128)
                            for cc in range(11):
                                c = half * 11 + cc
                                mm(b, ps[b][:], W_[:, cc, msl], hid[j][:, c, :], c == 0, c == NFT - 1,
                                   reads=[("w", s_, "a"), ("hid", j, c)], writes=[("R", j)])
                            if half == 1:
                                tt("dve", xb[j][:, n, :], ps[b][:], xb[j][:, n, :], ALU.add, reads=[("x", j, n)],
                                   writes=[("ps", b), ("x", j, n)])

            P.label = "final"
            goff = depth * PV_L
            for j in JS:
                rs = rms_stats(j)
                for k in range(8):
                    to = T32()
                    stt(t32[to][:], xb[j][:, k, :], pvec[:, goff + k:goff + k + 1], t32[rs][:], ALU.mult, ALU.mult,
                        reads=[("x", j, k), ("t", rs), "pvec"], writes=[("t", to)])
                    P.op("sp", lambda e, j=j, k=k, to=to, t0=tok0[j]: e.dma_start(out=outT[k * 128:(k + 1) * 128, t0:t0 + CB], in_=t32[to][:]),
                         reads=[("t", to)], dma_slot="out")
        P.wait_for("sp", [("dma:out", n_blk * NB * 8 - 1)])
        P.emit(block, sems, dsems)
        nc._prog = P
    return nc


def _consts():
    cm = np.zeros((14, 128, 128), np.float32)
    cm[0] = np.eye(128, dtype=np.float32)
    cm[1] = 1.0
    s = np.arange(128)[:, None]
    t = np.arange(128)[None, :]
    for g, w in enumerate((2, 4, 8, 16)):
        band = ((t - s >= 0) & (t - s <= w - 1)).astype(np.float32)
        cm[2 + g] = band / w - np.eye(128, dtype=np.float32)
        cnt = np.minimum(t + 1, w).astype(np.float32)
        cm[6 + g] = band / cnt - np.eye(128, dtype=np.float32)
        cm[10 + g] = ((s - t) >= (129 - w)).astype(np.float32) / w
    c32 = np.zeros((2, 128, 128), np.float32)
    c32[0] = np.eye(128, dtype=np.float32)
    c32[1] = (t <= s).astype(np.float32)
    return np.ascontiguousarray(cm.transpose(1, 0, 2)), np.ascontiguousarray(c32.transpose(1, 0, 2))


def _pack_params(inp, depth):
    pv = np.zeros((128, depth * PV_L + 8), np.float32)

    def cols(v):
        return np.asarray(v, np.float32).reshape(-1, 128).T

    for l in range(depth):
        o = l * PV_L
        pv[:, o + PV_NMIX:o + PV_NMIX + 8] = cols(inp["norm_mix"][l])
        pv[:, o + PV_BGATE:o + PV_BGATE + 24] = cols(inp["b_gate"][l])
        pv[:, o + PV_PSCALE:o + PV_PSCALE + 8] = cols(inp["pool_scale"][l])
        cdw = np.asarray(inp["conv_dw_w"][l], np.float32)
        pv[:, o + PV_CDW:o + PV_CDW + 124] = cdw.reshape(31, 4, 128).transpose(2, 1, 0).reshape(128, 124)
        pv[:, o + PV_CDB:o + PV_CDB + 4] = cols(inp["conv_dw_b"][l])
        pv[:, o + PV_CLG:o + PV_CLG + 4] = cols(inp["conv_ln_g"][l])
        pv[:, o + PV_CLB:o + PV_CLB + 4] = cols(inp["conv_ln_b"][l])
        pv[:, o + PV_NFFN:o + PV_NFFN + 8] = cols(inp["norm_ffn"][l])
        fdw = np.asarray(inp["ffn_dw_w"][l], np.float32)
        pv[:, o + PV_FDW:o + PV_FDW + 66] = fdw.reshape(3, NFT, 128).transpose(2, 1, 0).reshape(128, 66)
        pv[:, o + PV_FDB:o + PV_FDB + NFT] = cols(inp["ffn_dw_b"][l])
    pv[:, depth * PV_L:depth * PV_L + 8] = cols(inp["norm_final"])
    return pv


def run_module(inp, n_cores, seqs_per_core, depth, trace=False):
    x = np.asarray(inp["x"], np.float32)
    B, S, _ = x.shape
    assert B == n_cores * seqs_per_core and S % CB == 0
    n_blk = S // CB
    nc = build_nc(seqs_per_core, n_blk, depth)
    cm, c32 = _consts()
    pv = _pack_params(inp, depth)
    sgb = np.stack([np.broadcast_to(np.asarray(inp["sgu_ln_g"], np.float32)[:depth, None, :], (depth, 128, 512)),
                    np.broadcast_to(np.asarray(inp["sgu_ln_b"], np.float32)[:depth, None, :], (depth, 128, 512))], axis=1)
    sgb = np.ascontiguousarray(sgb)
    sbrow = np.ascontiguousarray(np.asarray(inp["sgu_b"], np.float32)[:depth].reshape(depth, 1, 512))
    shared = {
        "w_in": np.ascontiguousarray(inp["w_in"][:depth], np.float32),
        "pool_w": np.ascontiguousarray(inp["pool_w"][:depth], np.float32),
        "conv_pw": np.ascontiguousarray(inp["conv_pw"][:depth], np.float32),
        "sgu_w": np.ascontiguousarray(inp["sgu_w"][:depth], np.float32),
        "sgu_out": np.ascontiguousarray(inp["sgu_out"][:depth], np.float32),
        "w_out": np.ascontiguousarray(inp["w_out"][:depth], np.float32),
        "ffn_up": np.ascontiguousarray(inp["ffn_up"][:depth], np.float32),
        "ffn_down": np.ascontiguousarray(inp["ffn_down"][:depth], np.float32),
        "pvec": pv, "sgb": sgb, "sbrow": sbrow, "cmat": cm, "c32": c32,
    }
    in_maps = []
    for c in range(n_cores):
        xs = x[c * seqs_per_core:(c + 1) * seqs_per_core].reshape(seqs_per_core * S, D)
        m = dict(shared)
        m["xT"] = np.ascontiguousarray(xs.T)
        in_maps.append(m)
    res = run_bass_kernel_spmd(nc, in_maps, core_ids=list(range(n_cores)), trace=trace)
    outs = []
    for c in range(n_cores):
        o = np.asarray(res.results[c]["outT"], np.float32).T.reshape(seqs_per_core, S, D)
        outs.append(o)
    out = np.concatenate(outs, axis=0)
    return out, res


def kernel(**inputs):
    out, _ = run_module(inputs, 8, 2, 4)
    return out.astype(np.float32)
```
